# Optimizing a Trainium2 kernel written in Bass

```python
import math
import jax
import jax.numpy as jnp
from jax import lax
import numpy as np

D_MODEL = 2048
BATCH = 1
SEQ = 16384
DEPTH = 1

DIFF_WIDTH = D_MODEL // 2
SB_WIDTH = D_MODEL - DIFF_WIDTH
DIFF_QK_DIM = 64
DIFF_V_DIM = 2 * DIFF_QK_DIM
DIFF_HEADS = DIFF_WIDTH // DIFF_V_DIM
SB_HEAD_DIM = 128
SB_HEADS = SB_WIDTH // SB_HEAD_DIM
D_FF = 4 * D_MODEL
ROPE_THETA = 500000.0
ROPE_FRACTION_DEN = 4
BLOCK_Q = 128
NORM_EPS = 1e-6
NEG_INF = -1e30
PROJ_SPLITS = (
    2 * DIFF_HEADS * DIFF_QK_DIM,
    2 * DIFF_HEADS * DIFF_QK_DIM,
    DIFF_HEADS * DIFF_V_DIM,
    SB_HEADS * SB_HEAD_DIM,
    SB_HEADS * SB_HEAD_DIM,
    SB_HEADS * SB_HEAD_DIM,
)
PROJ_WIDTH = sum(PROJ_SPLITS)

kernel_name = "hymba_diff_stickbreaking_hybrid"


def rmsnorm(x, g):
    xf = x.astype(jnp.float32)
    y = xf * lax.rsqrt(jnp.mean(jnp.square(xf), axis=-1, keepdims=True) + NORM_EPS)
    return (y * g.astype(jnp.float32)).astype(x.dtype)


def partial_rotary(x, pos):
    d = x.shape[-1]
    rot = d // ROPE_FRACTION_DEN
    half = rot // 2
    inv_freq = ROPE_THETA ** (-jnp.arange(0, rot, 2, dtype=jnp.float32) / rot)
    ang = pos.astype(jnp.float32)[:, None] * inv_freq[None, :]
    cos = jnp.cos(ang)[None, :, None, :]
    sin = jnp.sin(ang)[None, :, None, :]
    xr = x[..., :rot].astype(jnp.float32)
    x1, x2 = xr[..., :half], xr[..., half:]
    rotated = jnp.concatenate([x1 * cos - x2 * sin, x2 * cos + x1 * sin], axis=-1)
    return jnp.concatenate([rotated.astype(x.dtype), x[..., rot:]], axis=-1)


def to_blocks(a):
    b, s = a.shape[0], a.shape[1]
    return jnp.swapaxes(a.reshape(b, s // BLOCK_Q, BLOCK_Q, *a.shape[2:]), 0, 1)


def from_blocks(a):
    a = jnp.swapaxes(a, 0, 1)
    return a.reshape(a.shape[0], a.shape[1] * a.shape[2], *a.shape[3:])


def differential_attention(q1, q2, k1, k2, v, lam):
    s_len = q1.shape[1]
    key_pos = jnp.arange(s_len)
    scale = DIFF_QK_DIM ** -0.5

    def block(args):
        qb1, qb2, qpos = args
        causal = key_pos[None, :] <= qpos[:, None]

        def probs(qb, k):
            sc = jnp.einsum("bqhd,bkhd->bhqk", qb, k,
                            preferred_element_type=jnp.float32) * scale
            return jax.nn.softmax(jnp.where(causal, sc, NEG_INF), axis=-1)

        w = probs(qb1, k1) - lam * probs(qb2, k2)
        return jnp.einsum("bhqk,bkhd->bqhd", w.astype(v.dtype), v)

    qpos = jnp.arange(s_len).reshape(-1, BLOCK_Q)
    out = lax.map(block, (to_blocks(q1), to_blocks(q2), qpos))
    return from_blocks(out)


def stick_breaking_attention(q, k, v):
    s_len = q.shape[1]
    key_pos = jnp.arange(s_len)
    scale = SB_HEAD_DIM ** -0.5

    def block(args):
        qb, qpos = args
        strict = key_pos[None, :] < qpos[:, None]
        z = jnp.einsum("bqhd,bkhd->bhqk", qb, k,
                       preferred_element_type=jnp.float32) * scale
        log_beta = jax.nn.log_sigmoid(z)
        log_one_minus = jnp.where(strict, jax.nn.log_sigmoid(-z), 0.0)
        log_stick = lax.cumsum(log_one_minus, axis=3, reverse=True) - log_one_minus
        a = jnp.where(strict, jnp.exp(log_beta + log_stick), 0.0)
        return jnp.einsum("bhqk,bkhd->bqhd", a.astype(v.dtype), v)

    qpos = jnp.arange(s_len).reshape(-1, BLOCK_Q)
    out = lax.map(block, (to_blocks(q), qpos))
    return from_blocks(out)


def setup_inputs(seed: int = 0) -> dict:
    key = jax.random.key(seed)
    ks = jax.random.split(key, 16)
    f32 = jnp.float32

    def normal(k, shape, scale):
        return jax.random.normal(k, shape, f32) * scale

    return {
        "x": normal(ks[0], (BATCH, SEQ, D_MODEL), 1.0),
        "ln1": 1.0 + normal(ks[1], (DEPTH, D_MODEL), 0.02),
        "w_in": normal(ks[2], (DEPTH, D_MODEL, PROJ_WIDTH), D_MODEL ** -0.5),
        "lambda_q1": normal(ks[3], (DEPTH, DIFF_QK_DIM), 0.1),
        "lambda_k1": normal(ks[4], (DEPTH, DIFF_QK_DIM), 0.1),
        "lambda_q2": normal(ks[5], (DEPTH, DIFF_QK_DIM), 0.1),
        "lambda_k2": normal(ks[6], (DEPTH, DIFF_QK_DIM), 0.1),
        "diff_head_norm": 1.0 + normal(ks[7], (DEPTH, DIFF_V_DIM), 0.02),
        "sb_head_norm": 1.0 + normal(ks[8], (DEPTH, SB_HEAD_DIM), 0.02),
        "w_out": normal(ks[9], (DEPTH, DIFF_WIDTH + SB_WIDTH, D_MODEL), (DIFF_WIDTH + SB_WIDTH) ** -0.5),
        "ln2": 1.0 + normal(ks[10], (DEPTH, D_MODEL), 0.02),
        "w_mlp_in": normal(ks[11], (DEPTH, D_MODEL, D_FF), D_MODEL ** -0.5),
        "w_mlp_out": normal(ks[12], (DEPTH, D_FF, D_MODEL), 0.2 * D_FF ** -0.5),
        "ln_f": 1.0 + normal(ks[13], (D_MODEL,), 0.02),
    }


def reference(x, ln1, w_in, lambda_q1, lambda_k1, lambda_q2, lambda_k2,
              diff_head_norm, sb_head_norm, w_out, ln2, w_mlp_in, w_mlp_out, ln_f):
    b, s_len, _ = x.shape
    pos = jnp.arange(s_len)
    split_points = list(np.cumsum(PROJ_SPLITS)[:-1])
    for l in range(DEPTH):
        lam_init = 0.8 - 0.6 * math.exp(-0.3 * l)

        h = rmsnorm(x, ln1[l])
        proj = h @ w_in[l]
        dq, dk, dv, sq, sk, sv = jnp.split(proj, split_points, axis=-1)

        dq = dq.reshape(b, s_len, DIFF_HEADS, 2, DIFF_QK_DIM)
        dk = dk.reshape(b, s_len, DIFF_HEADS, 2, DIFF_QK_DIM)
        q1 = partial_rotary(dq[..., 0, :], pos)
        q2 = partial_rotary(dq[..., 1, :], pos)
        k1 = partial_rotary(dk[..., 0, :], pos)
        k2 = partial_rotary(dk[..., 1, :], pos)
        dv = dv.reshape(b, s_len, DIFF_HEADS, DIFF_V_DIM)
        lam = (jnp.exp(jnp.sum(lambda_q1[l].astype(jnp.float32) * lambda_k1[l].astype(jnp.float32)))
               - jnp.exp(jnp.sum(lambda_q2[l].astype(jnp.float32) * lambda_k2[l].astype(jnp.float32)))
               + lam_init)
        diff_out = differential_attention(q1, q2, k1, k2, dv, lam)
        diff_out = rmsnorm(diff_out, diff_head_norm[l]) * (1.0 - lam_init)

        sq = sq.reshape(b, s_len, SB_HEADS, SB_HEAD_DIM)
        sk = sk.reshape(b, s_len, SB_HEADS, SB_HEAD_DIM)
        sv = sv.reshape(b, s_len, SB_HEADS, SB_HEAD_DIM)
        sb_out = rmsnorm(stick_breaking_attention(sq, sk, sv), sb_head_norm[l])

        mixed = jnp.concatenate([diff_out.reshape(b, s_len, DIFF_WIDTH),
                                 sb_out.reshape(b, s_len, SB_WIDTH)], axis=-1)
        x = x + mixed @ w_out[l]

        h = rmsnorm(x, ln2[l])
        x = x + jnp.square(jax.nn.relu(h @ w_mlp_in[l])) @ w_mlp_out[l]
    return rmsnorm(x, ln_f)
```

```python
from contextlib import ExitStack
import math
import numpy as np
import ml_dtypes
import concourse.bass as bass
import concourse.mybir as mybir
from concourse.bass_utils import run_bass_kernel_spmd

F32 = mybir.dt.float32
BF16 = mybir.dt.bfloat16
AF = mybir.ActivationFunctionType
ALU = mybir.AluOpType
AX = mybir.AxisListType

ENGS = ("pe", "act", "dve", "pool", "sp")
NCORES = 8
D = 2048
DFF = 8192
EPS = 1e-6
NEG = -30000.0


class Res:
    __slots__ = ("name", "lw", "rd", "semid", "semval")

    def __init__(self, name):
        self.name = name
        self.lw = None
        self.rd = {}
        self.semid = None
        self.semval = 0


class Rec:
    def __init__(self):
        self.streams = {e: [] for e in ENGS}
        self.clock = {e: {} for e in ENGS}
        self.snap = {}
        self.marked = {e: set() for e in ENGS}
        self.ndsem = 0
        self.last = {}

    def _need(self, eng, deps):
        ck = self.clock[eng]
        waits = []
        for tok in sorted(deps, key=lambda t: (str(t[0]), -t[1])):
            key, val = tok
            if eng == "pe" and key == "pe":
                continue
            if ck.get(key, -1) >= val:
                continue
            waits.append(tok)
            sn = self.snap.get(tok)
            if sn:
                for k, v in sn.items():
                    if ck.get(k, -1) < v:
                        ck[k] = v
            if ck.get(key, -1) < val:
                ck[key] = val
        for key, val in waits:
            if not isinstance(key, tuple):
                self.marked[key].add(val)
        return waits

    @staticmethod
    def _deps(reads, writes):
        deps = set()
        for r in reads:
            if r.lw is not None:
                deps.add(r.lw)
        for w in writes:
            if w.lw is not None:
                deps.add(w.lw)
            deps.update(w.rd.values())
        return deps

    def op(self, eng, fn, reads=(), writes=()):
        waits = self._need(eng, self._deps(reads, writes))
        st = self.streams[eng]
        pos = len(st)
        tok = (eng, pos)
        st.append((fn, waits, None))
        sn = dict(self.clock[eng])
        sn[eng] = pos
        self.snap[tok] = sn
        self.last[eng] = tok
        for r in reads:
            r.rd[eng] = tok
        for w in writes:
            w.lw = tok
            w.rd = {}
        return tok

    def dma(self, eng, fn, reads=(), writes=(), semres=None):
        if semres is None:
            semres = writes[0] if writes else reads[0]
        if semres.semid is None:
            semres.semid = self.ndsem
            self.ndsem += 1
        waits = self._need(eng, self._deps(reads, writes))
        semres.semval += 16
        key = ("d", semres.semid)
        tok = (key, semres.semval)
        self.streams[eng].append((fn, waits, (semres.semid,)))
        sn = dict(self.clock[eng])
        sn[key] = semres.semval
        self.snap[tok] = sn
        self.last[key] = tok
        for r in reads:
            r.rd[key] = tok
        for w in writes:
            w.lw = tok
            w.rd = {}
        return tok

    def wait(self, eng, toks):
        waits = self._need(eng, set(t for t in toks if t is not None))
        if waits:
            self.streams[eng].append((None, waits, None))

    def barrier(self):
        toks = list(self.last.values())
        for e in ENGS:
            self.wait(e, toks)

    def emit(self, nc, stack):
        esem = {e: stack.enter_context(nc.semaphore("s_" + e)) for e in ENGS if e != "sp"}
        dsem = [stack.enter_context(nc.semaphore("d%d" % i)) for i in range(self.ndsem)]
        rank = {}
        for e in ENGS:
            for i, p in enumerate(sorted(self.marked[e])):
                rank[(e, p)] = i + 1
        handles = {"pe": "tensor", "act": "scalar", "dve": "vector", "pool": "gpsimd", "sp": "sync"}
        block = stack.enter_context(nc.Block())

        def run(ename):
            def body(engine):
                for pos, (fn, waits, dinfo) in enumerate(self.streams[ename]):
                    for key, val in waits:
                        if isinstance(key, tuple):
                            engine.wait_ge(dsem[key[1]], val)
                        else:
                            engine.wait_ge(esem[key], rank[(key, val)])
                    if fn is None:
                        continue
                    bi = fn(engine)
                    if dinfo is not None:
                        bi.then_inc(dsem[dinfo[0]], 16)
                    elif (ename, pos) in rank:
                        bi.then_inc(esem[ename], 1)
            return body

        for ename in ENGS:
            if self.streams[ename]:
                getattr(block, handles[ename])(run(ename))
        self.snap = None


class Arena:
    def __init__(self, ap, words):
        self.ap = ap
        self.words = words
        self.off = 0
        self.peak = 0

    def _take(self, words):
        o = self.off
        self.off += words
        self.peak = max(self.peak, self.off)
        assert self.off <= self.words, ("SBUF arena overflow", self.off, self.words)
        return o

    def f32(self, *shape):
        n = int(np.prod(shape))
        o = self._take(n)
        v = self.ap[:, o:o + n]
        if len(shape) == 2:
            v = v.rearrange("p (a b) -> p a b", b=shape[1])
        return v

    def bf16(self, *shape):
        n = int(np.prod(shape))
        assert n % 2 == 0
        o = self._take(n // 2)
        v = self.ap[:, o:o + n // 2].bitcast(BF16)
        if len(shape) == 2:
            v = v.rearrange("p (a b) -> p a b", b=shape[1])
        return v


CONST_COLS = 4 * 128 + 2 * 256 + 4 * 512


def make_consts():
    c = np.zeros((128, CONST_COLS), np.float32)
    j = np.arange(128)[:, None]
    s = np.arange(128)[None, :]
    c[:, 0:128] = 1.0
    c[:, 128:256] = (j == s)
    c[:, 256:384] = -1.0 * (j >= s)
    c[:, 384:512] = -1.0 * (j < s)
    o = 512
    for i in range(2):
        t = np.arange(256)[None, :]
        c[:, o:o + 256] = np.where(128 * i + j <= t, 0.0, NEG)
        o += 256
    for i in range(4):
        t = np.arange(512)[None, :]
        c[:, o:o + 512] = np.where(128 * i + j < t, 0.0, NEG)
        o += 512
    return c.astype(ml_dtypes.bfloat16)


def build(S, stop_after=99):
    TPC = S // NCORES
    assert TPC % 512 == 0
    NTT = S // 512
    NBLK = S // 128
    NT2 = TPC // 512
    nc = bass.Bass("TRN2", target_bir_lowering=False)
    dt = nc.dram_tensor

    def din(name, shape, dtype=F32):
        return dt(name, shape, dtype, kind="ExternalInput").ap()

    xT = din("xT", [D, S])
    win = din("win", [D, 1024])
    cs = din("cs", [2, 128, S])
    g1d = din("g1", [128, 16])
    g2d = din("g2", [128, 16])
    gfd = din("gf", [128, 16])
    lamp = din("lamp", [256])
    ghd = din("ghd", [2, 128])
    wout = din("wout", [D, D])
    w1 = din("w1", [D, DFF])
    w2 = din("w2", [DFF, D])
    cbd = din("cb", [128, CONST_COLS], BF16)
    xres = din("xres", [D, S // NCORES])
    yT = dt("yT", [D, TPC], F32, kind="ExternalOutput").ap()
    sc_qk = [dt("sc_qk%d" % i, [128, S], BF16).ap() for i in range(4)]
    sc_v = [dt("sc_v%d" % i, [128, NBLK * 128], BF16).ap() for i in range(2)]
    NWB = 72
    wsc = dt("wsc", [NWB * 128, 4096], BF16).ap()
    mixloc = dt("mixloc", [256, S], BF16)
    gat = dt("gat", [NCORES * 256, S], BF16)

    rec = Rec()
    st = ExitStack()
    with st:
        WORDS = 50 * 1024
        arena_t = st.enter_context(nc.sbuf_tensor("arena", [128, WORDS], F32))
        ar = Arena(arena_t[:, :], WORDS)
        ps = st.enter_context(nc.psum_tensor("ps", [128, 8, 512], F32))
        cc_sem = st.enter_context(nc.semaphore("cc"))
        PB = [Res("bank%d" % i) for i in range(8)]
        POOLS = {"all": list(range(8)), "hi": [4, 5, 6, 7], "s": [4, 5, 6], "t": [7], "t2": [2, 3], "pair": [4, 6]}
        bank_ctr = {k: 0 for k in POOLS}

        def nb(pool="all"):
            lst = POOLS[pool]
            b = lst[bank_ctr[pool] % len(lst)]
            bank_ctr[pool] += 1
            return b

        def mm(out, lhsT, rhs, start, stop, reads, writes):
            return rec.op("pe", lambda e: e.matmul(out, lhsT, rhs, start=start, stop=stop,
                                                   skip_group_check=True), reads, writes)

        def act(out, in_, func, reads, writes, scale=1.0, bias=0.0, accum_out=None):
            if accum_out is None:
                return rec.op("act", lambda e: e.activation(out, in_, func, bias=bias, scale=scale), reads, writes)
            return rec.op("act", lambda e: e.activation(out, in_, func, bias=bias, scale=scale,
                                                        accum_out=accum_out), reads, writes)

        def tt(eng, out, in0, in1, op, reads, writes):
            return rec.op(eng, lambda e: e.tensor_tensor(out, in0, in1, op), reads, writes)

        def ts(eng, out, in0, s1, s2, op0, op1, reads, writes):
            return rec.op(eng, lambda e: e.tensor_scalar(out, in0, s1, s2, op0, op1), reads, writes)

        def stt(eng, out, in0, scalar, in1, op0, op1, reads, writes):
            return rec.op(eng, lambda e: e.scalar_tensor_tensor(out, in0, scalar, in1, op0, op1), reads, writes)

        def cp(eng, out, in_, reads, writes):
            return rec.op(eng, lambda e: e.tensor_copy(out, in_), reads, writes)

        def dma(out, in_, reads, writes, eng="sp", semres=None, split=1):
            if split > 1:
                n = out.shape[1]
                assert n % split == 0
                st_ = n // split
                tk = None
                for i_ in range(split):
                    tk = dma(out[:, i_ * st_:(i_ + 1) * st_], in_[:, i_ * st_:(i_ + 1) * st_], reads, writes, eng, semres)
                return tk
            return rec.dma(eng, lambda e: e.dma_start(out=out, in_=in_), reads, writes, semres)

        cb = ar.bf16(CONST_COLS)
        R_cb = Res("cb")
        dma(cb, cbd, [], [R_cb])
        ones_b = cb[:, 0:128]
        ident_b = cb[:, 128:256]
        negtri_b = cb[:, 256:384]
        fix_b = cb[:, 384:512]
        maskD = [cb[:, 512 + 256 * i: 512 + 256 * (i + 1)] for i in range(2)]
        maskS = [cb[:, 1024 + 512 * i: 1024 + 512 * (i + 1)] for i in range(4)]

        g1 = ar.f32(16)
        g2 = ar.f32(16)
        gf = ar.f32(16)
        R_g = Res("gains")
        dma(g1, g1d, [], [R_g])
        dma(g2, g2d, [], [R_g])
        dma(gf, gfd, [], [R_g])
        ghn = ar.f32(2, 128)
        R_ghn = Res("ghn")
        dma(ghn[:, 0, :], ghd[0].partition_broadcast(128), [], [R_ghn])
        dma(ghn[:, 1, :], ghd[1].partition_broadcast(128), [], [R_ghn])
        lam_init = 0.8 - 0.6 * math.exp(-0.3 * 0)
        ts("dve", ghn[:, 0, :], ghn[:, 0, :], 1.0 - lam_init, None, ALU.mult, ALU.bypass, [R_ghn], [R_ghn])
        lpt = ar.f32(256)
        R_lp = Res("lp")
        dma(lpt, lamp.partition_broadcast(128), [], [R_lp])
        lsc = ar.f32(8)
        R_ls = Res("lsc")
        prod = ar.f32(128)
        R_prod = Res("prod")
        tt("dve", prod[:, 0:64], lpt[:, 0:64], lpt[:, 64:128], ALU.mult, [R_lp], [R_prod])
        tt("dve", prod[:, 64:128], lpt[:, 128:192], lpt[:, 192:256], ALU.mult, [R_lp], [R_prod])
        rec.op("dve", lambda e: e.reduce_sum(lsc[:, 0:1], prod[:, 0:64], axis=AX.X), [R_prod], [R_ls])
        rec.op("dve", lambda e: e.reduce_sum(lsc[:, 1:2], prod[:, 64:128], axis=AX.X), [R_prod], [R_ls])
        act(lsc[:, 2:4], lsc[:, 0:2], AF.Exp, [R_ls], [R_ls])
        tt("dve", lsc[:, 4:5], lsc[:, 2:3], lsc[:, 3:4], ALU.subtract, [R_ls], [R_ls])
        ts("dve", lsc[:, 5:6], lsc[:, 4:5], lam_init, -1.0, ALU.add, ALU.mult, [R_ls], [R_ls])
        nlam = lsc[:, 5:6]
        persist_mark = ar.off

        xt = [ar.f32(16, 512) for _ in range(2)]
        R_xt = [Res("xt0"), Res("xt1")]
        xn = [ar.bf16(16, 512) for _ in range(2)]
        R_xn = [Res("xn0"), Res("xn1")]
        wb = ar.bf16(16, 1024)
        R_wb = Res("wb")
        cst = [ar.f32(2, 512) for _ in range(2)]
        R_cst = [Res("cs0"), Res("cs1")]
        lnv = ar.f32(512)
        rstd = ar.f32(512)
        R_rstd = Res("rstd")
        t1 = ar.f32(512)
        t2 = ar.f32(512)
        R_t = Res("t12")
        stg = [[ar.bf16(512) for _ in range(6)] for _ in range(2)]
        R_stg = [[Res("stg%d_%d" % (s, i)) for i in range(6)] for s in range(2)]

        win_v = win.rearrange("(kc p) n -> p kc n", p=128)
        for h in range(2):
            dma(xt[h], win_v[:, :, h * 512:(h + 1) * 512], [], [R_xt[h]], split=4)
            tt("dve", wb[:, :, h * 512:(h + 1) * 512], xt[h],
               g1.unsqueeze(2).to_broadcast([128, 16, 512]), ALU.mult, [R_xt[h], R_g], [R_wb])

        xT_v = xT.rearrange("(kc p) s -> p kc s", p=128)
        R_scr = [Res("scr%d" % i) for i in range(6)]
        inv_sqrt_d = 1.0 / math.sqrt(D)
        def proj_A(t):
            sl = t % 2
            tok = slice(t * 512, (t + 1) * 512)
            dma(xt[sl], xT_v[:, :, tok], [], [R_xt[sl]], split=4)
            dma(cst[sl][:, 0, :], cs[0][:, tok], [], [R_cst[sl]])
            dma(cst[sl][:, 1, :], cs[1][:, tok], [], [R_cst[sl]])
            act(xn[sl], xt[sl], AF.Square, [R_xt[sl]], [R_xn[sl]], scale=inv_sqrt_d)
            b = nb()
            for kc in range(16):
                mm(ps[:, b, :], ones_b, xn[sl][:, kc, :], kc == 0, kc == 15, [R_cb, R_xn[sl]], [PB[b]])
            act(lnv, ps[:, b, :], AF.Ln, [PB[b]], [R_rstd], bias=EPS)
            act(rstd, lnv, AF.Exp, [R_rstd], [R_rstd], scale=-0.5)
            tt("dve", xn[sl], xt[sl], rstd.unsqueeze(1).to_broadcast([128, 16, 512]), ALU.mult,
               [R_xt[sl], R_rstd], [R_xn[sl]])

        def proj_fm(t, o):
            sl = t % 2
            b = nb()
            for kc in range(16):
                mm(ps[:, b, :], wb[:, kc, o * 128:(o + 1) * 128], xn[sl][:, kc, :], kc == 0, kc == 15,
                   [R_wb, R_xn[sl]], [PB[b]])
            return b

        def proj_rot(t, qi, braw, bsw):
            sl = t % 2
            tok = slice(t * 512, (t + 1) * 512)
            tt("dve", t1, ps[:, braw, :], cst[sl][:, 0, :], ALU.mult, [PB[braw], R_cst[sl]], [R_t])
            tt("dve", t2, ps[:, bsw, :], cst[sl][:, 1, :], ALU.mult, [PB[bsw], R_cst[sl]], [R_t])
            tt("dve", stg[sl][qi], t1, t2, ALU.add, [R_t], [R_stg[sl][qi]])
            dma(sc_qk[qi][:, tok], stg[sl][qi], [R_stg[sl][qi]], [R_scr[qi]], eng="pool", semres=R_stg[sl][qi])

        def proj_B1(t):
            b0 = proj_fm(t, 0)
            b1 = proj_fm(t, 1)
            proj_rot(t, 0, b0, b1)
            b2 = proj_fm(t, 2)
            b3 = proj_fm(t, 3)
            proj_rot(t, 1, b2, b3)

        def proj_B2(t):
            sl = t % 2
            tok = slice(t * 512, (t + 1) * 512)
            for qi, o in ((2, 4), (3, 5)):
                b = proj_fm(t, o)
                act(stg[sl][qi], ps[:, b, :], AF.Copy, [PB[b]], [R_stg[sl][qi]])
                dma(sc_qk[qi][:, tok], stg[sl][qi], [R_stg[sl][qi]], [R_scr[qi]], eng="pool", semres=R_stg[sl][qi])
            vb = [nb(), nb()]
            for tb in range(4):
                b = vb[tb // 2]
                o = (tb % 2) * 256
                for kc in range(16):
                    mm(ps[:, b, o:o + 256], xn[sl][:, kc, tb * 128:(tb + 1) * 128], wb[:, kc, 768:1024],
                       kc == 0, kc == 15, [R_wb, R_xn[sl]], [PB[b]])
            for hv in range(2):
                for half in range(2):
                    b = vb[half]
                    src = ps[:, b, :].rearrange("p (t c) -> p t c", c=256)[:, :, hv * 128:(hv + 1) * 128]
                    dst = stg[sl][4 + hv].rearrange("p (t c) -> p t c", c=128)[:, half * 2:half * 2 + 2, :]
                    act(dst, src, AF.Copy, [PB[b]], [R_stg[sl][4 + hv]])
                dma(sc_v[hv][:, t * 512:(t + 1) * 512], stg[sl][4 + hv], [R_stg[sl][4 + hv]], [R_scr[4 + hv]],
                    eng="pool", semres=R_stg[sl][4 + hv])

        proj_A(0)
        for t in range(NTT):
            proj_B1(t)
            if t + 1 < NTT:
                proj_A(t + 1)
            proj_B2(t)

        rec.barrier()
        if stop_after == 1:
            rec.emit(nc, st)
            return nc
        ar.off = persist_mark

        qT = ar.bf16(S)
        kT = ar.bf16(S)
        vA = ar.bf16(NBLK, 130)
        R_q = Res("qT")
        R_k = Res("kT")
        R_v = Res("vA")
        NPB = 3
        pbuf = [ar.bf16(512) for _ in range(NPB)]
        R_p = [Res("p%d" % i) for i in range(NPB)]
        fbuf = [ar.f32(512) for _ in range(2)]
        R_f = [Res("f%d" % i) for i in range(2)]
        sm = ar.f32(16)
        R_sm = Res("sm")
        u1 = ar.f32(128)
        R_u = Res("u")
        junk = ar.f32(128)
        R_junk = Res("junk")
        ob = ar.bf16(128)
        R_ob = Res("ob")
        mst = [ar.bf16(512) for _ in range(2)]
        R_mst = [Res("mst0"), Res("mst1")]
        R_mix = Res("mixloc")
        mixloc_ap = mixloc.ap()

        ob4 = [ar.bf16(128) for _ in range(4)]
        R_ob4 = [Res("ob4_%d" % i) for i in range(4)]

        def head_out(src_u, row0, q0, gvec, mslot, qcol, tpool="t", defer=None, oslot=None):
            o_, R_o = (ob, R_ob) if oslot is None else (ob4[oslot], R_ob4[oslot])
            tt("dve", junk, src_u, src_u, ALU.mult, [R_u], [R_junk])
            rec.op("dve", lambda e: e.reduce_sum(sm[:, 8:9], junk, axis=AX.X), [R_junk], [R_sm])
            act(sm[:, 9:10], sm[:, 8:9], AF.Ln, [R_sm], [R_sm], scale=1.0 / 128.0, bias=EPS)
            act(sm[:, 10:11], sm[:, 9:10], AF.Exp, [R_sm], [R_sm], scale=-0.5)
            stt("dve", o_, src_u, sm[:, 10:11], gvec, ALU.mult, ALU.mult, [R_u, R_sm, R_ghn], [R_o])

            def fin():
                b = nb(tpool)
                pt = ps[:, b, 0:64].bitcast(BF16)
                rec.op("pe", lambda e: e.transpose(pt, o_, ident_b), [R_o, R_cb], [PB[b]])
                cp("dve", mst[mslot][:, qcol:qcol + 128], pt, [PB[b]], [R_mst[mslot]])
            if defer is None:
                fin()
            else:
                defer.append(fin)

        wout_v = wout.rearrange("(kc p) n -> p kc n", p=128)
        w1_v = w1.rearrange("(kc p) n -> p kc n", p=128)
        w2_v = w2.rearrange("(fc p) n -> p fc n", p=128)
        wjobs = []
        for jg in range(8):
            wjobs.append((wout_v[:, :, jg * 256:(jg + 1) * 256], "wo"))
        for fg in range(32):
            wjobs.append((w1_v[:, :, fg * 256:(fg + 1) * 256], "w1"))
        for jg in range(8):
            for fg in range(4):
                wjobs.append((w2_v[:, fg * 16:(fg + 1) * 16, jg * 256:(jg + 1) * 256], "w2"))
        assert len(wjobs) == NWB
        cws = [ar.f32(16, 256) for _ in range(2)]
        R_cws = [Res("cws0"), Res("cws1")]
        cwb = [ar.bf16(16, 256) for _ in range(2)]
        R_cwb = [Res("cwb0"), Res("cwb1")]
        R_wsc = Res("wsc")
        wjob_ctr = [0]

        def emit_wjob():
            j = wjob_ctr[0]
            if j >= NWB:
                return
            wjob_ctr[0] += 1
            src, kind = wjobs[j]
            sl_ = j % 2
            dma(cws[sl_], src, [], [R_cws[sl_]], split=2)
            if kind == "w1":
                tt("pool", cwb[sl_], cws[sl_], g2.unsqueeze(2).to_broadcast([128, 16, 256]), ALU.mult,
                   [R_cws[sl_], R_g], [R_cwb[sl_]])
            else:
                cp("pool", cwb[sl_], cws[sl_], [R_cws[sl_]], [R_cwb[sl_]])
            dma(wsc[j * 128:(j + 1) * 128, :], cwb[sl_].rearrange("p a b -> p (a b)"), [R_cwb[sl_]], [R_wsc],
                semres=R_cwb[sl_])

        dma(qT, sc_qk[0], [R_scr[0]], [R_q])
        dma(kT, sc_qk[1], [R_scr[1]], [R_k])
        rec.op("pool", lambda e: e.memset(vA[:, :, 128:130], 1.0), [], [R_v])
        dma(vA[:, :, 0:128], sc_v[0].rearrange("p (b c) -> p b c", c=128), [R_scr[4]], [R_v], split=NBLK // 8)
        NQT = S // 256
        ditems = [(T, kb) for T in range(NQT) for kb in range(2 * T + 2)]
        ND = len(ditems)

        def d_qk(k):
            T, kb = ditems[k]
            bs = 4 + 2 * (k % 2)
            i = kb - 2 * T
            ksl = slice(kb * 128, (kb + 1) * 128)
            qsl = slice(T * 256, (T + 1) * 256)
            for m in range(2):
                prt = slice(64 * m, 64 * (m + 1))
                out = ps[:, bs + m, 0:256]
                mm(out, kT[prt, ksl], qT[prt, qsl], True, i < 0, [R_k, R_q], [PB[bs + m]])
                if i >= 0:
                    mm(out, ident_b, maskD[i], False, True, [R_cb], [PB[bs + m]])

        def d_exp(k):
            bs = 4 + 2 * (k % 2)
            p = k % NPB
            act(pbuf[p].rearrange("p (m q) -> p m q", q=256), ps[:, bs:bs + 2, 0:256], AF.Exp,
                [PB[bs], PB[bs + 1]], [R_p[p]], scale=0.125)

        pending_d = []

        def d_pv(k):
            T, kb = ditems[k]
            p = k % NPB
            bo1, bo2 = 0, 1
            for m, bo in ((0, bo1), (1, bo2)):
                for qb in range(2):
                    if kb > 2 * T + qb:
                        continue
                    mm(ps[:, bo, qb * 130:qb * 130 + 129], pbuf[p][:, 256 * m + 128 * qb:256 * m + 128 * (qb + 1)],
                       vA[:, kb, 0:129], kb == 0 and qb == 0, kb == 2 * T + qb, [R_p[p], R_v], [PB[bo]])
            if kb != 2 * T + 1:
                return
            ms = (T // 2) % 2
            for qb in range(2):
                c0 = qb * 130
                rec.op("dve", lambda e, c0=c0: e.reciprocal(sm[:, 0:1], ps[:, bo1, c0 + 128:c0 + 129]),
                       [PB[bo1]], [R_sm])
                rec.op("dve", lambda e, c0=c0: e.reciprocal(sm[:, 1:2], ps[:, bo2, c0 + 128:c0 + 129]),
                       [PB[bo2]], [R_sm])
                tt("dve", sm[:, 2:3], sm[:, 1:2], nlam, ALU.mult, [R_sm, R_ls], [R_sm])
                ts("dve", u1, ps[:, bo1, c0:c0 + 128], sm[:, 0:1], None, ALU.mult, ALU.bypass, [PB[bo1], R_sm], [R_u])
                stt("dve", u1, ps[:, bo2, c0:c0 + 128], sm[:, 2:3], u1, ALU.mult, ALU.add, [PB[bo2], R_sm, R_u], [R_u])
                head_out(u1, 0, T * 256 + qb * 128, ghn[:, 0, :], ms, (T % 2) * 256 + qb * 128, "t2",
                         defer=pending_d, oslot=(2 * T + qb) % 4)
            if T % 2 == 1:
                pending_d.append(lambda T=T, ms=ms: dma(mixloc_ap[0:128, (T - 1) * 256:(T + 1) * 256], mst[ms],
                                                        [R_mst[ms]], [R_mix], semres=R_mst[ms]))

        wstep = max(1, ND // (NWB + 2))
        d_qk(0)
        for k in range(ND):
            if k % wstep == 0:
                emit_wjob()
            if k + 1 < ND:
                d_qk(k + 1)
            d_exp(k)
            flush = list(pending_d)
            del pending_d[:]
            d_pv(k)
            for f_ in flush:
                f_()
        for f_ in pending_d:
            f_()
        while wjob_ctr[0] < NWB:
            emit_wjob()
        rec.barrier()
        if stop_after == 2:
            rec.emit(nc, st)
            return nc

        dma(qT, sc_qk[2], [R_scr[2]], [R_q])
        dma(kT, sc_qk[3], [R_scr[3]], [R_k])
        dma(vA[:, :, 0:128], sc_v[1].rearrange("p (b c) -> p b c", c=128), [R_scr[5]], [R_v], split=NBLK // 8)
        sb_scale = 1.0 / math.sqrt(128.0)
        NQT = S // 512
        sitems = [(T, kb) for T in range(NQT) for kb in range(4 * T + 3, -1, -1)]
        NS_ = len(sitems)
        ebuf = [ar.f32(512) for _ in range(4)]
        R_e = [Res("e%d" % i) for i in range(4)]
        spb = [ar.bf16(512) for _ in range(4)]
        R_sp = [Res("sp%d" % i) for i in range(4)]

        def s_qk(k):
            T, kb = sitems[k]
            bz = 4 + k % 3
            i = kb - 4 * T
            mm(ps[:, bz, :], kT[:, kb * 128:(kb + 1) * 128], qT[:, T * 512:(T + 1) * 512], True, i < 0,
               [R_k, R_q], [PB[bz]])
            if i >= 0:
                mm(ps[:, bz, :], ident_b, maskS[i], False, True, [R_cb], [PB[bz]])

        def s_esp(k):
            bz = 4 + k % 3
            s4 = k % 4
            act(ebuf[s4], ps[:, bz, :], AF.Exp, [PB[bz]], [R_e[s4]], scale=sb_scale)
            act(spb[s4], ebuf[s4], AF.Ln, [R_e[s4]], [R_sp[s4]], bias=1.0)

        def s_tri(k):
            T, kb = sitems[k]
            mm(ps[:, T % 2, :], negtri_b, spb[k % 4], kb == 4 * T + 3, False, [R_cb, R_sp[k % 4]], [PB[T % 2]])

        def s_f(k):
            T, kb = sitems[k]
            act(fbuf[k % 2], ps[:, T % 2, :], AF.Exp, [PB[T % 2]], [R_f[k % 2]])

        def s_fix(k):
            T, kb = sitems[k]
            mm(ps[:, T % 2, :], fix_b, spb[k % 4], False, False, [R_cb, R_sp[k % 4]], [PB[T % 2]])

        def s_A(k):
            tt("dve", pbuf[k % NPB], ebuf[k % 4], fbuf[k % 2], ALU.mult, [R_e[k % 4], R_f[k % 2]], [R_p[k % NPB]])

        def s_pv(k):
            T, kb = sitems[k]
            bo = 2 + T % 2
            p = k % NPB
            for qb in range(4):
                if kb > 4 * T + qb:
                    continue
                mm(ps[:, bo, qb * 128:(qb + 1) * 128], pbuf[p][:, qb * 128:(qb + 1) * 128], vA[:, kb, 0:128],
                   kb == 4 * T + 3 and qb == 3, kb == 0, [R_p[p], R_v], [PB[bo]])
            if kb != 0:
                return
            ms = T % 2
            for qb in range(4):
                cp("dve", u1, ps[:, bo, qb * 128:(qb + 1) * 128], [PB[bo]], [R_u])
                head_out(u1, 128, T * 512 + qb * 128, ghn[:, 1, :], ms, qb * 128)
            dma(mixloc_ap[128:256, T * 512:(T + 1) * 512], mst[ms], [R_mst[ms]], [R_mix], semres=R_mst[ms])

        for k in range(min(2, NS_)):
            s_qk(k)
            s_esp(k)
        for k in range(NS_):
            s_tri(k)
            s_f(k)
            if k + 2 < NS_:
                s_qk(k + 2)
            if k >= 1:
                s_pv(k - 1)
            s_fix(k)
            if k + 2 < NS_:
                s_esp(k + 2)
            s_A(k)
        s_pv(NS_ - 1)
        rec.barrier()
        if stop_after == 3:
            rec.emit(nc, st)
            return nc

        R_gat = Res("gat")

        def cc(e):
            bi = e.collective_compute("AllGather", ALU.bypass, replica_groups=[list(range(NCORES))],
                                      ins=[mixloc.ap().opt()], outs=[gat.ap().opt()])
            bi.then_inc(cc_sem)
            return e.wait_ge(cc_sem, 1)
        rec.op("pool", cc, [R_mix], [R_gat])
        rec.barrier()
        if stop_after == 4:
            rec.emit(nc, st)
            return nc
        ar.off = persist_mark

        x1 = ar.f32(16, 512)
        R_x1 = Res("x1")
        hb = ar.bf16(16, 512)
        R_hb = Res("hb")
        sq2 = ar.bf16(16, 512)
        R_sq2 = Res("sq2")
        ab_off = ar.off
        ab = ar.bf16(64, 512)
        R_ab = Res("ab")
        ostage = ar.ap[:, ab_off:ab_off + 8192].rearrange("p (a b) -> p a b", b=512)
        rbuf = [ar.f32(512) for _ in range(2)]
        R_rb = [Res("rb0"), Res("rb1")]
        NWS = 4
        wbb = [ar.bf16(16, 256) for _ in range(NWS)]
        R_wbb = [Res("wbb%d" % i) for i in range(NWS)]
        lnv2 = ar.f32(512)
        rstd2 = ar.f32(512)
        R_rstd2 = Res("rstd2")
        R_y = Res("yT")
        R_osem = Res("osem")
        gat_v = gat.ap().rearrange("(j p) s -> p j s", p=128)
        yT_v = yT.rearrange("(j p) s -> p j s", p=128)
        xres_v = xres.rearrange("(j p) s -> p j s", p=128)
        wctr = [0]
        gat_dyn = {}
        out_toks = []

        def wblock():
            i_ = wctr[0]
            wctr[0] += 1
            j = i_ % NWB
            s_ = i_ % NWS
            dma(wbb[s_].rearrange("p a b -> p (a b)"), wsc[j * 128:(j + 1) * 128, :], [R_wsc], [R_wbb[s_]])
            return wbb[s_], R_wbb[s_]

        def norm_stats(dst_rstd):
            b = nb("hi")
            for j in range(16):
                mm(ps[:, b, :], ones_b, sq2[:, j, :], j == 0, j == 15, [R_cb, R_sq2], [PB[b]])
            act(lnv2, ps[:, b, :], AF.Ln, [PB[b]], [R_rstd2], bias=EPS)
            act(dst_rstd, lnv2, AF.Exp, [R_rstd2], [R_rstd2], scale=-0.5)

        def load_mix(t2i):
            def ld_mix(e, h4=0):
                if "v" not in gat_dyn:
                    c = e.partition_id()
                    gat_dyn["v"] = gat_v[:, :, bass.ds(c * TPC, TPC)]
                gv = gat_dyn["v"]
                return e.dma_start(out=hb[:, h4 * 4:(h4 + 1) * 4, :],
                                   in_=gv[:, h4 * 4:(h4 + 1) * 4, t2i * 512:(t2i + 1) * 512])
            for h4 in range(4):
                rec.dma("pool", (lambda e, f=ld_mix, h4=h4: f(e, h4=h4)), [R_gat], [R_hb])

        def load_x(t2i):
            dma(x1, xres_v[:, :, t2i * 512:(t2i + 1) * 512], [], [R_x1], eng="pool", split=4)

        load_mix(0)
        load_x(0)
        for t2i in range(NT2):
            tok = slice(t2i * 512, (t2i + 1) * 512)
            for jg in range(8):
                wt, R_wt = wblock()
                for jj in range(2):
                    j = jg * 2 + jj
                    b = nb("hi")
                    for kc in range(16):
                        mm(ps[:, b, :], wt[:, kc, jj * 128:(jj + 1) * 128], hb[:, kc, :], kc == 0, kc == 15,
                           [R_wt, R_hb], [PB[b]])
                    tt("dve", x1[:, j, :], ps[:, b, :], x1[:, j, :], ALU.add, [PB[b], R_x1], [R_x1])
                    act(sq2[:, j, :], x1[:, j, :], AF.Square, [R_x1], [R_sq2], scale=inv_sqrt_d)
            norm_stats(rstd2)
            tt("dve", hb, x1, rstd2.unsqueeze(1).to_broadcast([128, 16, 512]), ALU.mult, [R_x1, R_rstd2], [R_hb])
            rctr = 0
            for fg in range(32):
                wt, R_wt = wblock()
                for ff in range(2):
                    f = fg * 2 + ff
                    b = nb("hi")
                    for kc in range(16):
                        mm(ps[:, b, :], wt[:, kc, ff * 128:(ff + 1) * 128], hb[:, kc, :], kc == 0, kc == 15,
                           [R_wt, R_hb], [PB[b]])
                    r = rctr % 2
                    rctr += 1
                    act(rbuf[r], ps[:, b, :], AF.Relu, [PB[b]], [R_rb[r]])
                    tt("pool", ab[:, f, :], rbuf[r], rbuf[r], ALU.mult, [R_rb[r]], [R_ab])
            if t2i + 1 < NT2:
                load_mix(t2i + 1)
            for jg in range(8):
                bks = [0, 1] if jg % 2 == 0 else [2, 3]
                for fg in range(4):
                    wt, R_wt = wblock()
                    for jj in range(2):
                        b = bks[jj]
                        for fc in range(16):
                            mm(ps[:, b, :], wt[:, fc, jj * 128:(jj + 1) * 128], ab[:, fg * 16 + fc, :],
                               fg == 0 and fc == 0, fg == 3 and fc == 15, [R_wt, R_ab], [PB[b]])
                for jj in range(2):
                    j = jg * 2 + jj
                    b = bks[jj]
                    tt("dve", x1[:, j, :], ps[:, b, :], x1[:, j, :], ALU.add, [PB[b], R_x1], [R_x1])
                    act(sq2[:, j, :], x1[:, j, :], AF.Square, [R_x1], [R_sq2], scale=inv_sqrt_d)
            norm_stats(rstd2)
            for j in range(16):
                stt("dve", ostage[:, j, :], x1[:, j, :], gf[:, j:j + 1], rstd2, ALU.mult, ALU.mult,
                    [R_x1, R_g, R_rstd2], [R_ab])
            if t2i + 1 < NT2:
                load_x(t2i + 1)
            out_toks.append(dma(yT_v[:, :, tok], ostage, [R_ab], [R_y], eng="pool", semres=R_osem, split=4))
        rec.wait("sp", out_toks)
        rec.barrier()
        rec.emit(nc, st)
        print("SBUF peak words", ar.peak, "of", WORDS)
    return nc


def rope_tables(S):
    rot = 16
    inv_freq = (500000.0 ** (-np.arange(0, rot, 2, dtype=np.float32) / rot)).astype(np.float32)
    ang = np.arange(S, dtype=np.float32)[None, :] * inv_freq[:, None]
    cos = np.cos(ang).astype(np.float32)
    sin = np.sin(ang).astype(np.float32)
    cs = np.zeros((2, 128, S), np.float32)
    cs[0] = 1.0
    for base in (0, 64):
        cs[0, base:base + 8] = cos
        cs[0, base + 8:base + 16] = cos
        cs[1, base:base + 8] = -sin
        cs[1, base + 8:base + 16] = sin
    return cs


_SWAP = np.arange(128)
for _b in (0, 64):
    _SWAP[_b:_b + 8] = np.arange(_b + 8, _b + 16)
    _SWAP[_b + 8:_b + 16] = np.arange(_b, _b + 8)


def make_in_maps(x, ln1, w_in, lambda_q1, lambda_k1, lambda_q2, lambda_k2, diff_head_norm, sb_head_norm,
                 w_out, ln2, w_mlp_in, w_mlp_out, ln_f):
    x = np.asarray(x, np.float32)
    S = x.shape[1]
    TPC = S // NCORES
    xT = np.ascontiguousarray(x[0].T)
    cs = rope_tables(S)
    wi = np.asarray(w_in, np.float32)[0]

    def pk(v):
        return np.ascontiguousarray(np.asarray(v, np.float32).reshape(16, 128).T)
    g1, g2, gf = pk(ln1[0]), pk(ln2[0]), pk(ln_f)
    lamp = np.concatenate([np.asarray(a, np.float32)[0] for a in (lambda_q1, lambda_k1, lambda_q2, lambda_k2)])
    ghd = np.stack([np.asarray(diff_head_norm, np.float32)[0], np.asarray(sb_head_norm, np.float32)[0]])
    wo = np.asarray(w_out, np.float32)[0]
    perm = np.concatenate([np.concatenate([np.arange(128 * r, 128 * (r + 1)),
                                           np.arange(1024 + 128 * r, 1024 + 128 * (r + 1))]) for r in range(NCORES)])
    wout_p = np.ascontiguousarray(wo[perm])
    w1 = np.ascontiguousarray(np.asarray(w_mlp_in, np.float32)[0])
    w2 = np.ascontiguousarray(np.asarray(w_mlp_out, np.float32)[0])
    cbv = make_consts()
    maps = []
    for c in range(NCORES):
        cols = []
        for base in (0, 1024):
            blk = wi[:, base + 128 * c: base + 128 * (c + 1)]
            cols += [blk, blk[:, _SWAP]]
        for base in (3072, 4096, 2048, 5120):
            cols.append(wi[:, base + 128 * c: base + 128 * (c + 1)])
        win_c = np.ascontiguousarray(np.concatenate(cols, axis=1))
        maps.append({"xT": xT, "win": win_c, "cs": cs, "g1": g1, "g2": g2, "gf": gf, "lamp": lamp, "ghd": ghd,
                     "wout": wout_p, "w1": w1, "w2": w2, "cb": cbv,
                     "xres": np.ascontiguousarray(xT[:, c * TPC:(c + 1) * TPC])})
    return maps


_NC_CACHE = {}


def kernel(**inputs):
    S = inputs["x"].shape[1]
    if S not in _NC_CACHE:
        _NC_CACHE[S] = build(S)
    nc = _NC_CACHE[S]
    maps = make_in_maps(**inputs)
    res = run_bass_kernel_spmd(nc, maps, core_ids=list(range(NCORES)))
    yT = np.concatenate([np.asarray(r["yT"]) for r in res.results], axis=1)
    return np.ascontiguousarray(yT.T)[None].astype(np.float32)
```
